# Optimizing a Trainium2 kernel written in Bass

```python
import jax, jax.numpy as jnp
from jax import lax
import numpy as np

D_MODEL = 1024
BATCH = 8
SEQ = 4096
DEPTH = 4

N_A_LAYERS = DEPTH // 2
N_B_LAYERS = DEPTH - N_A_LAYERS
CONV_WIDTH = 3
D_FF = 2816
N_HEADS = 8
QK_NOPE_DIM = 128
QK_ROPE_DIM = 64
V_HEAD_DIM = 128
Q_LORA_RANK = 512
KV_LORA_RANK = 256
ROPE_THETA = 10000.0
RMS_EPS = 1e-6
Q_BLOCK = 128

kernel_name = "yoco_shortconv_mla_convffn_trunk"


def rms_norm(x, g):
    xf = x.astype(jnp.float32)
    y = xf * lax.rsqrt(jnp.mean(xf * xf, axis=-1, keepdims=True) + RMS_EPS)
    return (y * g.astype(jnp.float32)).astype(x.dtype)


def causal_dwconv(x, w):
    c = x.shape[-1]
    return lax.conv_general_dilated(
        x, w[:, None, :].astype(x.dtype), window_strides=(1,),
        padding=((CONV_WIDTH - 1, 0),), dimension_numbers=("NWC", "WIO", "NWC"),
        feature_group_count=c)


def rope_tables(seq, dtype):
    inv = 1.0 / (ROPE_THETA ** (jnp.arange(0, QK_ROPE_DIM, 2, dtype=jnp.float32) / QK_ROPE_DIM))
    ang = jnp.arange(seq, dtype=jnp.float32)[:, None] * inv[None, :]
    return jnp.cos(ang).astype(dtype), jnp.sin(ang).astype(dtype)


def apply_rope(x, cos, sin):
    x1, x2 = jnp.split(x, 2, axis=-1)
    return jnp.concatenate([x1 * cos - x2 * sin, x1 * sin + x2 * cos], axis=-1)


def short_conv_mixer(x_n, w_in, conv_w, w_out):
    b, c, h = jnp.split(x_n @ w_in, 3, axis=-1)
    return (b * causal_dwconv(c * h, conv_w)) @ w_out


def conv_ffn(x_n, w_up, conv_w, w_down):
    g, u = jnp.split(causal_dwconv(x_n @ w_up, conv_w), 2, axis=-1)
    return (jax.nn.silu(g) * u) @ w_down


def shared_latent_kv(h, kv_in_norm, w_dkv, kv_norm, w_ukv, cos, sin):
    bsz, seq, _ = h.shape
    ckv = rms_norm(h, kv_in_norm) @ w_dkv
    c_kv, k_rope = ckv[..., :KV_LORA_RANK], ckv[..., KV_LORA_RANK:]
    k_rope = apply_rope(k_rope, cos, sin)
    kv = (rms_norm(c_kv, kv_norm) @ w_ukv).reshape(bsz, seq, N_HEADS, QK_NOPE_DIM + V_HEAD_DIM)
    return kv[..., :QK_NOPE_DIM], kv[..., QK_NOPE_DIM:], k_rope


def mla_attention(x_n, w_dq, q_norm, w_uq, w_o, k_nope, v, k_rope, cos, sin):
    bsz, seq, _ = x_n.shape
    q = (rms_norm(x_n @ w_dq, q_norm) @ w_uq).reshape(bsz, seq, N_HEADS, QK_NOPE_DIM + QK_ROPE_DIM)
    q_nope = q[..., :QK_NOPE_DIM]
    q_rope = apply_rope(q[..., QK_NOPE_DIM:], cos[:, None, :], sin[:, None, :])
    n_blk = seq // Q_BLOCK

    def to_blocks(t):
        return jnp.moveaxis(t.reshape(bsz, n_blk, Q_BLOCK, *t.shape[2:]), 1, 0)

    k_pos = jnp.arange(seq)
    scale = (QK_NOPE_DIM + QK_ROPE_DIM) ** -0.5

    def attend_block(args):
        qn, qr, blk = args
        s = (jnp.einsum('bqhd,bkhd->bhqk', qn, k_nope)
             + jnp.einsum('bqhr,bkr->bhqk', qr, k_rope)).astype(jnp.float32) * scale
        q_pos = blk * Q_BLOCK + jnp.arange(Q_BLOCK)
        s = jnp.where(k_pos[None, :] <= q_pos[:, None], s, -jnp.inf)
        p = jax.nn.softmax(s, axis=-1).astype(v.dtype)
        return jnp.einsum('bhqk,bkhd->bqhd', p, v)

    o = lax.map(attend_block, (to_blocks(q_nope), to_blocks(q_rope), jnp.arange(n_blk)))
    o = jnp.moveaxis(o, 0, 1).reshape(bsz, seq, N_HEADS * V_HEAD_DIM)
    return o @ w_o


def setup_inputs(seed: int = 0) -> dict:
    key = jax.random.key(seed)
    ks = jax.random.split(key, 24)
    f32 = jnp.float32

    def w(k, shape, fan_in):
        return jax.random.normal(k, shape, f32) * (fan_in ** -0.5)

    def gain(k, shape):
        return 1.0 + 0.02 * jax.random.normal(k, shape, f32)

    return {
        "x": jax.random.normal(ks[0], (BATCH, SEQ, D_MODEL), f32),
        "a_mix_norm": gain(ks[1], (N_A_LAYERS, D_MODEL)),
        "a_w_in": w(ks[2], (N_A_LAYERS, D_MODEL, 3 * D_MODEL), D_MODEL),
        "a_conv": w(ks[3], (N_A_LAYERS, CONV_WIDTH, D_MODEL), CONV_WIDTH),
        "a_w_out": w(ks[4], (N_A_LAYERS, D_MODEL, D_MODEL), D_MODEL),
        "b_mix_norm": gain(ks[5], (N_B_LAYERS, D_MODEL)),
        "b_w_dq": w(ks[6], (N_B_LAYERS, D_MODEL, Q_LORA_RANK), D_MODEL),
        "b_q_norm": gain(ks[7], (N_B_LAYERS, Q_LORA_RANK)),
        "b_w_uq": w(ks[8], (N_B_LAYERS, Q_LORA_RANK, N_HEADS * (QK_NOPE_DIM + QK_ROPE_DIM)), Q_LORA_RANK),
        "b_w_o": w(ks[9], (N_B_LAYERS, N_HEADS * V_HEAD_DIM, D_MODEL), N_HEADS * V_HEAD_DIM),
        "kv_in_norm": gain(ks[10], (D_MODEL,)),
        "w_dkv": w(ks[11], (D_MODEL, KV_LORA_RANK + QK_ROPE_DIM), D_MODEL),
        "kv_norm": gain(ks[12], (KV_LORA_RANK,)),
        "w_ukv": w(ks[13], (KV_LORA_RANK, N_HEADS * (QK_NOPE_DIM + V_HEAD_DIM)), KV_LORA_RANK),
        "ffn_norm": gain(ks[14], (DEPTH, D_MODEL)),
        "ffn_w_up": w(ks[15], (DEPTH, D_MODEL, 2 * D_FF), D_MODEL),
        "ffn_conv": w(ks[16], (DEPTH, CONV_WIDTH, 2 * D_FF), CONV_WIDTH),
        "ffn_w_down": w(ks[17], (DEPTH, D_FF, D_MODEL), D_FF),
        "final_norm": gain(ks[18], (D_MODEL,)),
    }


def reference(x, a_mix_norm, a_w_in, a_conv, a_w_out, b_mix_norm, b_w_dq, b_q_norm, b_w_uq, b_w_o,
              kv_in_norm, w_dkv, kv_norm, w_ukv, ffn_norm, ffn_w_up, ffn_conv, ffn_w_down, final_norm):
    cos, sin = rope_tables(x.shape[1], x.dtype)
    h = x
    shared = None
    for layer in range(DEPTH):
        if layer < N_A_LAYERS:
            h = h + short_conv_mixer(rms_norm(h, a_mix_norm[layer]), a_w_in[layer],
                                     a_conv[layer], a_w_out[layer])
        else:
            j = layer - N_A_LAYERS
            k_nope, v, k_rope = shared
            h = h + mla_attention(rms_norm(h, b_mix_norm[j]), b_w_dq[j], b_q_norm[j], b_w_uq[j],
                                  b_w_o[j], k_nope, v, k_rope, cos, sin)
        h = h + conv_ffn(rms_norm(h, ffn_norm[layer]), ffn_w_up[layer], ffn_conv[layer], ffn_w_down[layer])
        if layer == N_A_LAYERS - 1:
            shared = shared_latent_kv(h, kv_in_norm, w_dkv, kv_norm, w_ukv, cos, sin)
    return rms_norm(h, final_norm)
```

```python
import contextlib
import numpy as np
import concourse.bass as bass
import concourse.mybir as mybir
from concourse.bass_utils import run_bass_kernel_spmd

F32 = mybir.dt.float32
BF16 = mybir.dt.bfloat16
AF = mybir.ActivationFunctionType
ALU = mybir.AluOpType

D = 1024
S = 4096
NT = 512
DFF = 2816
NH = 8
EPS = 1e-6
SCALE = 192.0 ** -0.5
NEG = -30000.0
RING = 5

ENGS = ("pe", "act", "dve", "pool", "sp")


class Chan:
    def __init__(self, sem):
        self.sem = sem
        self.n = 0


class Ins:
    __slots__ = ("eng", "fn", "deps", "marked", "tick", "chan", "idx")

    def __init__(self, eng, fn, chan):
        self.eng = eng
        self.fn = fn
        self.chan = chan
        self.deps = []
        self.marked = False
        self.tick = 0
        self.idx = 0


class Prog:
    def __init__(self, nc, stack):
        self.nc = nc
        self.stack = stack
        self.streams = {e: [] for e in ENGS}
        self.last_w = {}
        self.readers = {}
        self.sems = {}
        for e in ENGS:
            self.sems[e] = stack.enter_context(nc.semaphore("tl_" + e))

    def chan(self, name):
        return Chan(self.stack.enter_context(self.nc.semaphore(name)))

    def op(self, eng, fn, reads=(), writes=(), chan=None):
        ins = Ins(eng, fn, chan)
        st = self.streams[eng]
        ins.idx = len(st)
        cand = []
        for r in reads:
            w = self.last_w.get(r)
            if w is not None:
                cand.append(w)
        for r in writes:
            w = self.last_w.get(r)
            if w is not None:
                cand.append(w)
            cand.extend(self.readers.get(r, ()))
        best = {}
        for d in cand:
            if d.chan is not None:
                key = d.chan
            else:
                if d.eng == "pe" and eng == "pe" and chan is None:
                    continue
                key = d.eng
            b = best.get(key)
            if b is None or (d.chan is None and d.idx > b.idx) or (d.chan is not None and d.tick > b.tick):
                best[key] = d
        if chan is not None:
            ins.tick = 16 * (chan.n + 1)
            chan.n += 1
        ins.deps = list(best.values())
        for d in ins.deps:
            d.marked = True
        for r in reads:
            self.readers.setdefault(r, []).append(ins)
        for r in writes:
            self.last_w[r] = ins
            self.readers[r] = []
        st.append(ins)
        return ins

    def emit(self):
        nc = self.nc
        for e in ENGS:
            c = 0
            for ins in self.streams[e]:
                if ins.chan is None and ins.marked:
                    c += 1
                    ins.tick = c
        sems = self.sems
        streams = self.streams

        def run(name, eng):
            waited = {}
            for ins in streams[name]:
                if ins.chan is not None and ins.tick > 16:
                    k = ins.chan
                    if waited.get(k, 0) < ins.tick - 16:
                        eng.wait_ge(k.sem, ins.tick - 16)
                        waited[k] = ins.tick - 16
                for d in ins.deps:
                    if d.chan is not None:
                        k, s = d.chan, d.chan.sem
                    else:
                        k, s = d.eng, sems[d.eng]
                    if waited.get(k, 0) < d.tick:
                        eng.wait_ge(s, d.tick)
                        waited[k] = d.tick
                r = ins.fn(eng)
                if ins.chan is not None:
                    r.then_inc(ins.chan.sem, 16)
                elif ins.marked:
                    r.then_inc(sems[name], 1)

        with nc.Block() as block:
            @block.tensor
            def _(e):
                run("pe", e)

            @block.scalar
            def _(e):
                run("act", e)

            @block.vector
            def _(e):
                run("dve", e)

            @block.gpsimd
            def _(e):
                run("pool", e)

            @block.sync
            def _(e):
                run("sp", e)


def _cols(c0, n=128):
    return np.arange(c0, c0 + n)


def ffn_pieces(l):
    L = []
    parts = ((0, 8), (8, 16), (16, 22))
    for (j0, j1) in parts:
        for j in range(j0, j1):
            L.append(("ffn_w_up", l, 0, 8, _cols(128 * j)))
            L.append(("ffn_w_up", l, 0, 8, _cols(DFF + 128 * j)))
        for m in range(8):
            L.append(("ffn_w_down", l, 128 * j0, j1 - j0, _cols(128 * m)))
    return L


def piece_list(lo, hi):
    L = []
    for layer in range(lo, hi):
        if layer < 2:
            l = layer
            for i in range(8):
                L.append(("a_w_in", l, 0, 8, _cols(1024 + 128 * i)))
                L.append(("a_w_in", l, 0, 8, _cols(2048 + 128 * i)))
                L.append(("a_w_in", l, 0, 8, _cols(128 * i)))
            for m in range(8):
                L.append(("a_w_out", l, 0, 8, _cols(128 * m)))
        else:
            if layer == 2:
                L.append(("w_dkv", None, 0, 8, _cols(0)))
                L.append(("w_dkv", None, 0, 8, _cols(128)))
                rope = np.arange(256, 320)
                rot = np.concatenate([np.arange(288, 320), np.arange(256, 288)])
                L.append(("w_dkv", None, 0, 8, np.concatenate([rope, rope])))
                L.append(("w_dkv", None, 0, 8, np.concatenate([rot, rot])))
                for h in range(NH):
                    L.append(("w_ukv", None, 0, 2, _cols(256 * h)))
                for g in range(2):
                    cols = np.concatenate([_cols(256 * h + 128) for h in range(4 * g, 4 * g + 4)])
                    L.append(("w_ukv", None, 0, 2, cols))
            j = layer - 2
            for m in range(4):
                L.append(("b_w_dq", j, 0, 8, _cols(128 * m)))
            for h in range(NH):
                L.append(("b_w_uq", j, 0, 4, _cols(192 * h)))
            for p in range(4):
                cols = np.concatenate([_cols(192 * h + 128, 64) for h in (2 * p, 2 * p + 1)])
                L.append(("b_w_uq", j, 0, 4, cols))
            for p in range(4):
                cols = np.concatenate([np.concatenate([_cols(192 * h + 160, 32), _cols(192 * h + 128, 32)])
                                       for h in (2 * p, 2 * p + 1)])
                L.append(("b_w_uq", j, 0, 4, cols))
            for m in range(8):
                L.append(("b_w_o", j, 0, 8, _cols(128 * m)))
        L.extend(ffn_pieces(layer))
    return L


def pack_pieces(specs, weights):
    out = np.zeros((len(specs), 128, 1024), np.float32)
    for i, (key, l, r0, kcs, cols) in enumerate(specs):
        W = weights[key]
        if l is not None:
            W = W[l]
        nc_ = len(cols)
        blk = W[r0:r0 + kcs * 128][:, cols]
        out[i, :, :kcs * nc_] = blk.reshape(kcs, 128, nc_).transpose(1, 0, 2).reshape(128, kcs * nc_)
    return out


def vec_layout():
    off = {}
    n = 0

    def add(name, cnt):
        nonlocal n
        off[name] = n
        n += cnt

    for l in range(2):
        add(("a_mix_norm", l), 8)
    for l in range(4):
        add(("ffn_norm", l), 8)
    for l in range(2):
        add(("b_mix_norm", l), 8)
    add(("kv_in_norm", None), 8)
    add(("final_norm", None), 8)
    for l in range(2):
        add(("b_q_norm", l), 4)
    add(("kv_norm", None), 2)
    for l in range(2):
        for tap in range(3):
            add(("a_conv", l, tap), 8)
    for l in range(4):
        for tap in range(3):
            add(("ffn_conv", l, tap), 44)
    return off, n


def pack_vec(weights):
    off, n = vec_layout()
    v = np.zeros((128, n), np.float32)
    for key, o in off.items():
        name, l = key[0], key[1]
        a = weights[name]
        if l is not None:
            a = a[l]
        if len(key) == 3:
            a = a[key[2]]
        c = a.shape[0] // 128
        v[:, o:o + c] = a.reshape(c, 128).T
    return v


def rope_tables():
    inv = (1.0 / (np.float32(10000.0) ** (np.arange(0, 64, 2, dtype=np.float32) / np.float32(64)))).astype(np.float32)
    ang = (np.arange(S, dtype=np.float32)[:, None] * inv[None, :]).astype(np.float32)
    cos = np.cos(ang).astype(np.float32).T
    sin = np.sin(ang).astype(np.float32).T
    cos4 = np.concatenate([cos, cos, cos, cos], axis=0)
    sin4 = np.concatenate([-sin, sin, -sin, sin], axis=0)
    return np.ascontiguousarray(np.stack([cos4, sin4], axis=0))


def const_tables():
    ident = np.eye(128, dtype=np.float32)
    k = np.arange(128)[:, None]
    q = np.arange(128)[None, :]
    negm = np.where(k > q, np.float32(NEG), np.float32(0.0)).astype(np.float32)
    return np.ascontiguousarray(np.concatenate([ident, negm], axis=1))


def build_nc(lo=0, hi=4, ntiles=8):
    specs = piece_list(lo, hi)
    NP = len(specs)
    voff, NV = vec_layout()
    nc = bass.Bass("TRN2", target_bir_lowering=False)
    xT = nc.dram_tensor("xT", [D, S], F32, kind="ExternalInput").ap()
    wp = nc.dram_tensor("wp", [NP, 128, 1024], F32, kind="ExternalInput").ap()
    vecd = nc.dram_tensor("vec", [128, NV], F32, kind="ExternalInput").ap()
    ropd = nc.dram_tensor("rope", [2, 128, S], F32, kind="ExternalInput").ap()
    cstd = nc.dram_tensor("cst", [128, 256], F32, kind="ExternalInput").ap()
    outT = nc.dram_tensor("outT", [D, S], F32, kind="ExternalOutput").ap()
    xT3 = xT.rearrange("(c p) s -> p c s", p=128)
    oT3 = outT.rearrange("(c p) s -> p c s", p=128)

    with contextlib.ExitStack() as st:
        P = Prog(nc, st)

        def sb(name, shape, dt):
            return st.enter_context(nc.sbuf_tensor(name, shape, dt))

        need_kv = hi > 2
        if need_kv:
            Kc = sb("Kc", [128, NH, S], BF16)
            Vc = sb("Vc", [128, S // 128, NH * 128], BF16)
            KR = sb("KR", [128, S], BF16)
        h = sb("h", [128, 8, NT], F32)
        xn = sb("xn", [128, 8, NT], BF16)
        R = sb("R", [128, 12, NT], BF16)
        ring = sb("ring", [128, RING, 1024], BF16)
        B16 = sb("B16", [128, 3, NT], BF16)
        rr = sb("rr", [128, NT], F32)
        cs = sb("cs", [128, 2, NT], F32)
        TU = sb("TU", [128, 2, NT + 2], F32)
        TC = sb("TC", [128, 3, NT], F32)
        cst = sb("cst_sb", [128, 6, 128], BF16)
        vec = sb("vec_sb", [128, NV], F32)
        halo_f = sb("halo_f", [128, 4 * 44, 2], F32)
        halo_a = sb("halo_a", [128, 2 * 8, 2], F32)
        ps = st.enter_context(nc.psum_tensor("ps", [128, 8, 512], F32))

        IDENT, NEGM, O1024, O512, O256, ONE = range(6)

        ring_ch = [P.chan("ring%d" % i) for i in range(RING)]
        ch_xs = [P.chan("ch_x%d" % i) for i in range(8)]
        ch_outs = [P.chan("ch_o%d" % i) for i in range(8)]
        ch_cs = P.chan("ch_cs")
        ch_vec = P.chan("ch_vec")
        ch_cst = P.chan("ch_cst")

        total_pieces = NP * ntiles
        state = {"pi": 0, "bank": 0, "b16": 0, "tu": 0, "tc": 0}
        held = set()

        def fetch(gi):
            if gi >= total_pieces:
                return
            k = gi % NP
            slot = gi % RING
            key, l, r0, kcs, cols = specs[k]
            n = kcs * len(cols)
            P.op("pool", lambda e: e.dma_start(out=ring[:, slot, 0:n], in_=wp[k, :, 0:n]),
                 writes=[("ring", slot)], chan=ring_ch[slot])

        def next_piece(expect_key):
            gi = state["pi"]
            state["pi"] += 1
            k = gi % NP
            assert specs[k][0] == expect_key, (specs[k][0], expect_key)
            return gi, gi % RING, specs[k]

        def bank(hold=False):
            for _ in range(16):
                b = state["bank"]
                state["bank"] = (b + 1) % 8
                if b not in held:
                    if hold:
                        held.add(b)
                    return b
            raise RuntimeError("no psum bank")

        def release(b):
            held.discard(b)

        def rot(name, n):
            v = state[name]
            state[name] = (v + 1) % n
            return v

        def vcol(key, c=0):
            o = voff[key] + c
            return vec[:, o:o + 1]

        def proj(expect_key, rhs_list, rhs_res, width=NT, hold=False, col0=0):
            gi, slot, spec = next_piece(expect_key)
            kcs = spec[3]
            assert kcs == len(rhs_list)
            b = bank(hold)
            for kc in range(kcs):
                rhs = rhs_list[kc]
                P.op("pe", (lambda e, kc=kc, rhs=rhs: e.matmul(
                    ps[:, b, col0:col0 + width], lhsT=ring[:, slot, kc * 128:(kc + 1) * 128], rhs=rhs,
                    start=(kc == 0), stop=(kc == kcs - 1))),
                    reads=[("ring", slot), rhs_res[kc]], writes=[("ps", b)])
            fetch(gi + RING)
            return b

        def norm(src_fn, src_res, nchunks, ones_idx, gain_key, dst_fn, dst_res, src_is_psum=False):
            bs = bank(True)
            for c in range(nchunks):
                q = rot("b16", 3)
                P.op("act", (lambda e, c=c, q=q: e.activation(out=B16[:, q, :], in_=src_fn(c), func=AF.Square)),
                     reads=[src_res(c)], writes=[("b16", q)])
                P.op("pe", (lambda e, c=c, q=q: e.matmul(ps[:, bs, :], lhsT=cst[:, ones_idx, :], rhs=B16[:, q, :],
                                                         start=(c == 0), stop=(c == nchunks - 1))),
                     reads=[("b16", q), "cst"], writes=[("ps", bs)])
            P.op("act", lambda e: e.activation(out=rr[:], in_=ps[:, bs, :], func=AF.Sqrt, bias=EPS, scale=1.0),
                 reads=[("ps", bs)], writes=["rr"])
            release(bs)
            P.op("dve", lambda e: e.reciprocal(out=rr[:], in_=rr[:]), reads=["rr"], writes=["rr"])
            for c in range(nchunks):
                P.op("dve", (lambda e, c=c: e.scalar_tensor_tensor(
                    out=dst_fn(c), in0=src_fn(c), scalar=vcol(gain_key, c), in1=rr[:],
                    op0=ALU.mult, op1=ALU.mult)),
                    reads=[src_res(c), "rr", "vec"], writes=[dst_res(c)])

        h_fn = lambda c: h[:, c, :]
        h_res = lambda c: ("h", c)
        xn_fn = lambda c: xn[:, c, :]
        xn_res = lambda c: ("xn", c)
        xn_list = [xn[:, c, :] for c in range(8)]
        xn_rl = [("xn", c) for c in range(8)]

        def conv_from_psum(b, halo_ap, w0, w1, w2):
            u = rot("tu", 2)
            t = rot("tc", 3)
            P.op("act", lambda e: e.activation(out=TU[:, u, 0:2], in_=halo_ap, func=AF.Copy),
                 reads=["halo"], writes=[("tu", u)])
            P.op("act", lambda e: e.activation(out=TU[:, u, 2:NT + 2], in_=ps[:, b, :], func=AF.Copy),
                 reads=[("ps", b)], writes=[("tu", u)])
            P.op("act", lambda e: e.activation(out=halo_ap, in_=TU[:, u, NT:NT + 2], func=AF.Copy),
                 reads=[("tu", u)], writes=["halo"])
            P.op("act", lambda e: e.activation(out=TC[:, t, :], in_=ps[:, b, :], func=AF.Copy, scale=w2),
                 reads=[("ps", b), "vec"], writes=[("tc", t)])
            P.op("dve", lambda e: e.scalar_tensor_tensor(out=TC[:, t, :], in0=TU[:, u, 1:NT + 1], scalar=w1,
                                                         in1=TC[:, t, :], op0=ALU.mult, op1=ALU.add),
                 reads=[("tu", u), ("tc", t), "vec"], writes=[("tc", t)])
            P.op("dve", lambda e: e.scalar_tensor_tensor(out=TC[:, t, :], in0=TU[:, u, 0:NT], scalar=w0,
                                                         in1=TC[:, t, :], op0=ALU.mult, op1=ALU.add),
                 reads=[("tu", u), ("tc", t), "vec"], writes=[("tc", t)])
            return t

        def resid_add(b, m):
            P.op("dve", lambda e: e.tensor_tensor(out=h[:, m, :], in0=h[:, m, :], in1=ps[:, b, :], op=ALU.add),
                 reads=[("h", m), ("ps", b)], writes=[("h", m)])

        def ffn(l):
            norm(h_fn, h_res, 8, O1024, ("ffn_norm", l), xn_fn, xn_res)
            for (j0, j1) in ((0, 8), (8, 16), (16, 22)):
                for j in range(j0, j1):
                    tcs = []
                    for f in (j, 22 + j):
                        b = proj("ffn_w_up", xn_list, xn_rl)
                        hidx = l * 44 + f
                        tcs.append(conv_from_psum(b, halo_f[:, hidx, :],
                                                  vcol(("ffn_conv", l, 0), f), vcol(("ffn_conv", l, 1), f),
                                                  vcol(("ffn_conv", l, 2), f)))
                    tg, tu_ = tcs
                    P.op("act", lambda e, tg=tg: e.activation(out=TC[:, tg, :], in_=TC[:, tg, :], func=AF.Silu),
                         reads=[("tc", tg)], writes=[("tc", tg)])
                    jj = j - j0
                    P.op("dve", lambda e, tg=tg, tu_=tu_, jj=jj: e.tensor_tensor(
                        out=R[:, jj, :], in0=TC[:, tg, :], in1=TC[:, tu_, :], op=ALU.mult),
                        reads=[("tc", tg), ("tc", tu_)], writes=[("R", jj)])
                n = j1 - j0
                for m in range(8):
                    b = proj("ffn_w_down", [R[:, k, :] for k in range(n)], [("R", k) for k in range(n)])
                    resid_add(b, m)

        def a_mixer(l):
            norm(h_fn, h_res, 8, O1024, ("a_mix_norm", l), xn_fn, xn_res)
            for i in range(8):
                bc = proj("a_w_in", xn_list, xn_rl)
                bh = proj("a_w_in", xn_list, xn_rl)
                bb = proj("a_w_in", xn_list, xn_rl)
                t0 = rot("tc", 3)
                u = rot("tu", 2)
                hal = halo_a[:, l * 8 + i, :]
                P.op("act", lambda e, t0=t0, bc=bc: e.activation(out=TC[:, t0, :], in_=ps[:, bc, :], func=AF.Copy),
                     reads=[("ps", bc)], writes=[("tc", t0)])
                P.op("act", lambda e, u=u, hal=hal: e.activation(out=TU[:, u, 0:2], in_=hal, func=AF.Copy),
                     reads=["halo"], writes=[("tu", u)])
                P.op("dve", lambda e, u=u, t0=t0, bh=bh: e.tensor_tensor(
                    out=TU[:, u, 2:NT + 2], in0=TC[:, t0, :], in1=ps[:, bh, :], op=ALU.mult),
                    reads=[("tc", t0), ("ps", bh)], writes=[("tu", u)])
                P.op("act", lambda e, u=u, hal=hal: e.activation(out=hal, in_=TU[:, u, NT:NT + 2], func=AF.Copy),
                     reads=[("tu", u)], writes=["halo"])
                t1 = rot("tc", 3)
                w0, w1, w2 = (vcol(("a_conv", l, tap), i) for tap in range(3))
                P.op("act", lambda e, u=u, t1=t1, w2=w2: e.activation(
                    out=TC[:, t1, :], in_=TU[:, u, 2:NT + 2], func=AF.Copy, scale=w2),
                    reads=[("tu", u), "vec"], writes=[("tc", t1)])
                P.op("dve", lambda e, u=u, t1=t1, w1=w1: e.scalar_tensor_tensor(
                    out=TC[:, t1, :], in0=TU[:, u, 1:NT + 1], scalar=w1, in1=TC[:, t1, :],
                    op0=ALU.mult, op1=ALU.add),
                    reads=[("tu", u), ("tc", t1), "vec"], writes=[("tc", t1)])
                P.op("dve", lambda e, u=u, t1=t1, w0=w0: e.scalar_tensor_tensor(
                    out=TC[:, t1, :], in0=TU[:, u, 0:NT], scalar=w0, in1=TC[:, t1, :],
                    op0=ALU.mult, op1=ALU.add),
                    reads=[("tu", u), ("tc", t1), "vec"], writes=[("tc", t1)])
                P.op("dve", lambda e, t1=t1, bb=bb, i=i: e.tensor_tensor(
                    out=R[:, i, :], in0=TC[:, t1, :], in1=ps[:, bb, :], op=ALU.mult),
                    reads=[("tc", t1), ("ps", bb)], writes=[("R", i)])
            for m in range(8):
                b = proj("a_w_out", [R[:, k, :] for k in range(8)], [("R", k) for k in range(8)])
                resid_add(b, m)

        def rope_combine(b_r, b_t, out_ap, out_res):
            ta = rot("tc", 3)
            tb = rot("tc", 3)
            P.op("dve", lambda e: e.tensor_tensor(out=TC[:, ta, :], in0=cs[:, 0, :], in1=ps[:, b_r, :], op=ALU.mult),
                 reads=["cs", ("ps", b_r)], writes=[("tc", ta)])
            P.op("dve", lambda e: e.tensor_tensor(out=TC[:, tb, :], in0=cs[:, 1, :], in1=ps[:, b_t, :], op=ALU.mult),
                 reads=["cs", ("ps", b_t)], writes=[("tc", tb)])
            P.op("dve", lambda e: e.tensor_tensor(out=out_ap, in0=TC[:, ta, :], in1=TC[:, tb, :], op=ALU.add),
                 reads=[("tc", ta), ("tc", tb)], writes=[out_res])

        def kv_stage(t):
            c0 = t * NT
            norm(h_fn, h_res, 8, O1024, ("kv_in_norm", None), xn_fn, xn_res)
            b0 = proj("w_dkv", xn_list, xn_rl, hold=True)
            b1 = proj("w_dkv", xn_list, xn_rl, hold=True)
            br = proj("w_dkv", xn_list, xn_rl, hold=True)
            bt = proj("w_dkv", xn_list, xn_rl, hold=True)
            rope_combine(br, bt, KR[:, c0:c0 + NT], ("KR", t))
            release(br)
            release(bt)
            bl = [b0, b1]
            norm(lambda c: ps[:, bl[c], :], lambda c: ("ps", bl[c]), 2, O256, ("kv_norm", None),
                 lambda c: R[:, c, :], lambda c: ("R", c))
            release(b0)
            release(b1)
            ck = [R[:, c, :] for c in range(2)]
            ckr = [("R", c) for c in range(2)]
            for hh in range(NH):
                b = proj("w_ukv", ck, ckr)
                P.op("act", lambda e, b=b, hh=hh: e.activation(out=Kc[:, hh, c0:c0 + NT], in_=ps[:, b, :], func=AF.Copy),
                     reads=[("ps", b)], writes=[("Kc", hh, t)])
            for g in range(2):
                gi, slot, spec = next_piece("w_ukv")
                for s4 in range(4):
                    b = bank()
                    for kc in range(2):
                        P.op("pe", lambda e, b=b, kc=kc, s4=s4, slot=slot: e.matmul(
                            ps[:, b, :], lhsT=R[:, kc, s4 * 128:(s4 + 1) * 128],
                            rhs=ring[:, slot, kc * 512:(kc + 1) * 512], start=(kc == 0), stop=(kc == 1)),
                            reads=[("ring", slot), ("R", kc)], writes=[("ps", b)])
                    eng = "act" if (s4 % 2 == 0) else "dve"
                    if eng == "act":
                        P.op("act", lambda e, b=b, s4=s4, g=g: e.activation(
                            out=Vc[:, 4 * t + s4, g * 512:(g + 1) * 512], in_=ps[:, b, :], func=AF.Copy),
                            reads=[("ps", b)], writes=[("Vc", 4 * t + s4, g)])
                    else:
                        P.op("dve", lambda e, b=b, s4=s4, g=g: e.tensor_copy(
                            out=Vc[:, 4 * t + s4, g * 512:(g + 1) * 512], in_=ps[:, b, :]),
                            reads=[("ps", b)], writes=[("Vc", 4 * t + s4, g)])
                fetch(gi + RING)

        def attention(t):
            nkc = 4 * (t + 1)

            def head(hh):
                p, half = hh // 2, hh % 2
                r0 = half * 64
                bo = bank(True)
                bd = bank(True)
                sb_ = {}
                pb_ = {}

                def s_mm(kc):
                    i = kc - 4 * t
                    q0 = max(0, i) * 128
                    b = bank(True)
                    sb_[kc] = b
                    diag = i >= 0
                    P.op("pe", lambda e: e.matmul(ps[:, b, q0:NT], lhsT=Kc[:, hh, kc * 128:(kc + 1) * 128],
                                                  rhs=R[:, hh, q0:NT], start=True, stop=False),
                         reads=[("Kc", hh, kc // 4), ("R", hh)], writes=[("ps", b)])
                    P.op("pe", lambda e: e.matmul(ps[:, b, q0:NT], lhsT=KR[r0:r0 + 64, kc * 128:(kc + 1) * 128],
                                                  rhs=R[r0:r0 + 64, 8 + p, q0:NT], start=False, stop=not diag),
                         reads=[("KR", kc // 4), ("R", 8 + p)], writes=[("ps", b)])
                    if diag:
                        P.op("pe", lambda e: e.matmul(ps[:, b, q0:q0 + 128], lhsT=cst[:, IDENT, :],
                                                      rhs=cst[:, NEGM, :], start=False, stop=True),
                             reads=["cst"], writes=[("ps", b)])

                def e_act(kc):
                    i = kc - 4 * t
                    q0 = max(0, i) * 128
                    b = sb_[kc]
                    q = rot("b16", 3)
                    pb_[kc] = q
                    P.op("act", lambda e: e.activation(out=B16[:, q, q0:NT], in_=ps[:, b, q0:NT], func=AF.Exp,
                                                       scale=SCALE),
                         reads=[("ps", b)], writes=[("b16", q)])
                    release(b)

                def pv_mm(kc):
                    i = kc - 4 * t
                    q0 = max(0, i) * 128
                    q = pb_[kc]
                    P.op("pe", lambda e: e.matmul(ps[:, bo, q0:NT], lhsT=Vc[:, kc, hh * 128:(hh + 1) * 128],
                                                  rhs=B16[:, q, q0:NT], start=(kc == 0), stop=(kc == nkc - 1)),
                         reads=[("Vc", kc, hh // 4), ("b16", q)], writes=[("ps", bo)])
                    P.op("pe", lambda e: e.matmul(ps[:, bd, q0:NT], lhsT=cst[:, ONE, :],
                                                  rhs=B16[:, q, q0:NT], start=(kc == 0), stop=(kc == nkc - 1)),
                         reads=["cst", ("b16", q)], writes=[("ps", bd)])

                s_mm(0)
                if nkc > 1:
                    s_mm(1)
                for kc in range(nkc):
                    e_act(kc)
                    pv_mm(kc)
                    if kc + 2 < nkc:
                        s_mm(kc + 2)
                P.op("dve", lambda e: e.reciprocal(out=rr[:], in_=ps[:, bd, :]), reads=[("ps", bd)], writes=["rr"])
                P.op("dve", lambda e: e.tensor_tensor(out=xn[:, hh, :], in0=rr[:], in1=ps[:, bo, :], op=ALU.mult),
                     reads=["rr", ("ps", bo)], writes=[("xn", hh)])
                release(bo)
                release(bd)

            for hh in range(NH):
                head(hh)

        def b_mixer(j, t):
            norm(h_fn, h_res, 8, O1024, ("b_mix_norm", j), xn_fn, xn_res)
            bq = [proj("b_w_dq", xn_list, xn_rl, hold=True) for _ in range(4)]
            norm(lambda c: ps[:, bq[c], :], lambda c: ("ps", bq[c]), 4, O512, ("b_q_norm", j),
                 lambda c: xn[:, c, :], lambda c: ("xn", c))
            for b in bq:
                release(b)
            ql = [xn[:, c, :] for c in range(4)]
            qlr = [("xn", c) for c in range(4)]
            for hh in range(NH):
                b = proj("b_w_uq", ql, qlr)
                P.op("act", lambda e, b=b, hh=hh: e.activation(out=R[:, hh, :], in_=ps[:, b, :], func=AF.Copy),
                     reads=[("ps", b)], writes=[("R", hh)])
            brs = [proj("b_w_uq", ql, qlr, hold=True) for _ in range(4)]
            for p in range(4):
                bt = proj("b_w_uq", ql, qlr)
                rope_combine(brs[p], bt, R[:, 8 + p, :], ("R", 8 + p))
                release(brs[p])
            attention(t)
            for m in range(8):
                b = proj("b_w_o", xn_list, xn_rl)
                resid_add(b, m)

        P.op("sp", lambda e: e.dma_start(out=vec[:], in_=vecd), writes=["vec"], chan=ch_vec)
        P.op("pool", lambda e: e.dma_start(out=cst[:, 0:2, :], in_=cstd.rearrange("p (a b) -> p a b", a=2)),
             writes=["cst"], chan=ch_cst)
        for idx, val in ((O1024, 1.0 / 1024), (O512, 1.0 / 512), (O256, 1.0 / 256), (ONE, 1.0)):
            P.op("dve", lambda e, idx=idx, val=val: e.memset(cst[:, idx, :], val), writes=["cst"])
        P.op("dve", lambda e: e.memset(halo_f[:], 0.0), writes=["halo"])
        P.op("dve", lambda e: e.memset(halo_a[:], 0.0), writes=["halo"])
        for gi in range(RING):
            fetch(gi)

        for t in range(ntiles):
            c0 = t * NT
            for c in range(8):
                P.op("sp", lambda e, c0=c0, c=c: e.dma_start(out=h[:, c, :], in_=xT3[:, c, c0:c0 + NT]),
                     writes=[("h", c)], chan=ch_xs[c])
            if need_kv:
                P.op("sp", lambda e, c0=c0: e.dma_start(out=cs[:], in_=ropd[:, :, c0:c0 + NT].rearrange("a p s -> p a s")),
                     writes=["cs"], chan=ch_cs)
            for layer in range(lo, hi):
                if layer < 2:
                    a_mixer(layer)
                else:
                    if layer == 2:
                        kv_stage(t)
                    b_mixer(layer - 2, t)
                ffn(layer)
            if hi == 4:
                norm(h_fn, h_res, 8, O1024, ("final_norm", None), h_fn, h_res)
            for c in range(8):
                P.op("sp", lambda e, c0=c0, c=c: e.dma_start(out=oT3[:, c, c0:c0 + NT], in_=h[:, c, :]),
                     reads=[("h", c)], chan=ch_outs[c])
        assert state["pi"] == total_pieces, (state["pi"], total_pieces)
        lasts = P.streams["sp"][-8:]
        fin = P.op("sp", lambda e: e.nop())
        fin.deps = list(lasts)
        P.emit()
    return nc


def _host_inputs(inputs, lo, hi):
    specs = piece_list(lo, hi)
    w = {k: np.asarray(v, np.float32) for k, v in inputs.items() if k != "x"}
    wp = pack_pieces(specs, w)
    vec = pack_vec(w)
    rope = rope_tables()
    cst = const_tables()
    return wp, vec, rope, cst


def kernel(**inputs):
    x = np.asarray(inputs["x"], np.float32)
    B = x.shape[0]
    wp, vec, rope, cst = _host_inputs(inputs, 0, 4)
    nc = build_nc(0, 4, S // NT)
    in_maps = []
    for b in range(B):
        in_maps.append({"xT": np.ascontiguousarray(x[b].T), "wp": wp, "vec": vec, "rope": rope, "cst": cst})
    res = run_bass_kernel_spmd(nc, in_maps, core_ids=list(range(B)))
    out = np.stack([np.ascontiguousarray(res.results[b]["outT"].T) for b in range(B)], axis=0)
    return out.astype(np.float32)
```

```python
import contextlib
import numpy as np
import concourse.bass as bass
import concourse.mybir as mybir
from concourse.bass_utils import run_bass_kernel_spmd

F32 = mybir.dt.float32
BF16 = mybir.dt.bfloat16
AF = mybir.ActivationFunctionType
ALU = mybir.AluOpType

D = 1024
S = 4096
NT = 512
DFF = 2816
NH = 8
EPS = 1e-6
SCALE = 192.0 ** -0.5
NEG = -30000.0
RING = 5
NWARM = 18
FFN_PARTS = ((0, 6), (6, 12), (12, 18), (18, 22))

ENGS = ("pe", "act", "dve", "pool", "sp")


class Chan:
    def __init__(self, sem):
        self.sem = sem
        self.n = 0


class Ins:
    __slots__ = ("eng", "fn", "deps", "marked", "tick", "chan", "idx")

    def __init__(self, eng, fn, chan):
        self.eng = eng
        self.fn = fn
        self.chan = chan
        self.deps = []
        self.marked = False
        self.tick = 0
        self.idx = 0


class Prog:
    def __init__(self, nc, stack):
        self.nc = nc
        self.stack = stack
        self.streams = {e: [] for e in ENGS}
        self.last_w = {}
        self.readers = {}
        self.sems = {}
        for e in ENGS:
            self.sems[e] = stack.enter_context(nc.semaphore("tl_" + e))

    def chan(self, name):
        return Chan(self.stack.enter_context(self.nc.semaphore(name)))

    def op(self, eng, fn, reads=(), writes=(), chan=None):
        ins = Ins(eng, fn, chan)
        st = self.streams[eng]
        ins.idx = len(st)
        cand = []
        for r in reads:
            w = self.last_w.get(r)
            if w is not None:
                cand.append(w)
        for r in writes:
            w = self.last_w.get(r)
            if w is not None:
                cand.append(w)
            cand.extend(self.readers.get(r, ()))
        best = {}
        for d in cand:
            if d.chan is not None:
                key = d.chan
            else:
                if d.eng == "pe" and eng == "pe" and chan is None:
                    continue
                key = d.eng
            b = best.get(key)
            if b is None or (d.chan is None and d.idx > b.idx) or (d.chan is not None and d.tick > b.tick):
                best[key] = d
        if chan is not None:
            ins.tick = 16 * (chan.n + 1)
            chan.n += 1
        ins.deps = list(best.values())
        for d in ins.deps:
            d.marked = True
        for r in reads:
            self.readers.setdefault(r, []).append(ins)
        for r in writes:
            self.last_w[r] = ins
            self.readers[r] = []
        st.append(ins)
        return ins

    def emit(self):
        nc = self.nc
        for e in ENGS:
            c = 0
            for ins in self.streams[e]:
                if ins.chan is None and ins.marked:
                    c += 1
                    ins.tick = c
        sems = self.sems
        streams = self.streams

        def run(name, eng):
            waited = {}
            for ins in streams[name]:
                if ins.chan is not None and ins.tick > 16:
                    k = ins.chan
                    if waited.get(k, 0) < ins.tick - 16:
                        eng.wait_ge(k.sem, ins.tick - 16)
                        waited[k] = ins.tick - 16
                for d in ins.deps:
                    if d.chan is not None:
                        k, s = d.chan, d.chan.sem
                    else:
                        k, s = d.eng, sems[d.eng]
                    if waited.get(k, 0) < d.tick:
                        eng.wait_ge(s, d.tick)
                        waited[k] = d.tick
                r = ins.fn(eng)
                if ins.chan is not None:
                    r.then_inc(ins.chan.sem, 16)
                elif ins.marked:
                    r.then_inc(sems[name], 1)

        with nc.Block() as block:
            @block.tensor
            def _(e):
                run("pe", e)

            @block.scalar
            def _(e):
                run("act", e)

            @block.vector
            def _(e):
                run("dve", e)

            @block.gpsimd
            def _(e):
                run("pool", e)

            @block.sync
            def _(e):
                run("sp", e)


def _cols(c0, n=128):
    return np.arange(c0, c0 + n)


def ffn_schedule():
    n = len(FFN_PARTS)
    sched = [("up", 0)]
    for p in range(1, n):
        sched.append(("up", p))
        sched.append(("down", p - 1))
    sched.append(("down", n - 1))
    return sched


def ffn_pieces(l):
    L = []
    for kind, p in ffn_schedule():
        j0, j1 = FFN_PARTS[p]
        if kind == "up":
            for j in range(j0, j1):
                L.append(("ffn_w_up", l, 0, 8, _cols(128 * j)))
                L.append(("ffn_w_up", l, 0, 8, _cols(DFF + 128 * j)))
        else:
            for m in range(8):
                L.append(("ffn_w_down", l, 128 * j0, j1 - j0, _cols(128 * m)))
    return L


def piece_list(lo, hi):
    L = []
    for layer in range(lo, hi):
        if layer < 2:
            l = layer
            for i in range(8):
                L.append(("a_w_in", l, 0, 8, _cols(1024 + 128 * i)))
                L.append(("a_w_in", l, 0, 8, _cols(2048 + 128 * i)))
                L.append(("a_w_in", l, 0, 8, _cols(128 * i)))
            for m in range(8):
                L.append(("a_w_out", l, 0, 8, _cols(128 * m)))
        else:
            if layer == 2:
                L.append(("w_dkv", None, 0, 8, _cols(0)))
                L.append(("w_dkv", None, 0, 8, _cols(128)))
                rope = np.arange(256, 320)
                rot = np.concatenate([np.arange(288, 320), np.arange(256, 288)])
                L.append(("w_dkv", None, 0, 8, np.concatenate([rope, rope])))
                L.append(("w_dkv", None, 0, 8, np.concatenate([rot, rot])))
                for h in range(NH):
                    L.append(("w_ukv", None, 0, 2, _cols(256 * h)))
                for g in range(2):
                    cols = np.concatenate([_cols(256 * h + 128) for h in range(4 * g, 4 * g + 4)])
                    L.append(("w_ukv", None, 0, 2, cols))
            j = layer - 2
            for m in range(4):
                L.append(("b_w_dq", j, 0, 8, _cols(128 * m)))
            for h in range(NH):
                L.append(("b_w_uq", j, 0, 4, _cols(192 * h)))
            for p in range(4):
                cols = np.concatenate([_cols(192 * h + 128, 64) for h in (2 * p, 2 * p + 1)])
                L.append(("b_w_uq", j, 0, 4, cols))
            for p in range(4):
                cols = np.concatenate([np.concatenate([_cols(192 * h + 160, 32), _cols(192 * h + 128, 32)])
                                       for h in (2 * p, 2 * p + 1)])
                L.append(("b_w_uq", j, 0, 4, cols))
            for m in range(8):
                L.append(("b_w_o", j, 0, 8, _cols(128 * m)))
        L.extend(ffn_pieces(layer))
    return L


def pack_pieces(specs, weights):
    out = np.zeros((len(specs), 128, 1024), np.float32)
    for i, (key, l, r0, kcs, cols) in enumerate(specs):
        W = weights[key]
        if l is not None:
            W = W[l]
        nc_ = len(cols)
        blk = W[r0:r0 + kcs * 128][:, cols]
        out[i, :, :kcs * nc_] = blk.reshape(kcs, 128, nc_).transpose(1, 0, 2).reshape(128, kcs * nc_)
    return out


def vec_layout():
    off = {}
    n = 0

    def add(name, cnt):
        nonlocal n
        off[name] = n
        n += cnt

    for l in range(2):
        add(("a_mix_norm", l), 8)
    for l in range(4):
        add(("ffn_norm", l), 8)
    for l in range(2):
        add(("b_mix_norm", l), 8)
    add(("kv_in_norm", None), 8)
    add(("final_norm", None), 8)
    for l in range(2):
        add(("b_q_norm", l), 4)
    add(("kv_norm", None), 2)
    for l in range(2):
        for tap in range(3):
            add(("a_conv", l, tap), 8)
    for l in range(4):
        for tap in range(3):
            add(("ffn_conv", l, tap), 44)
    return off, n


def pack_vec(weights):
    off, n = vec_layout()
    v = np.zeros((128, n), np.float32)
    for key, o in off.items():
        name, l = key[0], key[1]
        a = weights[name]
        if l is not None:
            a = a[l]
        if len(key) == 3:
            a = a[key[2]]
        c = a.shape[0] // 128
        v[:, o:o + c] = a.reshape(c, 128).T
    return v


def rope_tables():
    inv = (1.0 / (np.float32(10000.0) ** (np.arange(0, 64, 2, dtype=np.float32) / np.float32(64)))).astype(np.float32)
    ang = (np.arange(S, dtype=np.float32)[:, None] * inv[None, :]).astype(np.float32)
    cos = np.cos(ang).astype(np.float32).T
    sin = np.sin(ang).astype(np.float32).T
    cos4 = np.concatenate([cos, cos, cos, cos], axis=0)
    sin4 = np.concatenate([-sin, sin, -sin, sin], axis=0)
    return np.ascontiguousarray(np.stack([cos4, sin4], axis=0))


def const_tables():
    ident = np.eye(128, dtype=np.float32)
    k = np.arange(128)[:, None]
    q = np.arange(128)[None, :]
    negm = np.where(k > q, np.float32(NEG), np.float32(0.0)).astype(np.float32)
    return np.ascontiguousarray(np.concatenate([ident, negm], axis=1))


def build_nc(lo=0, hi=4, ntiles=8):
    specs = piece_list(lo, hi)
    NP = len(specs)
    voff, NV = vec_layout()
    nc = bass.Bass("TRN2", target_bir_lowering=False)
    xT = nc.dram_tensor("xT", [D, S], F32, kind="ExternalInput").ap()
    wp = nc.dram_tensor("wp", [NP, 128, 1024], F32, kind="ExternalInput").ap()
    vecd = nc.dram_tensor("vec", [128, NV], F32, kind="ExternalInput").ap()
    ropd = nc.dram_tensor("rope", [2, 128, S], F32, kind="ExternalInput").ap()
    cstd = nc.dram_tensor("cst", [128, 256], F32, kind="ExternalInput").ap()
    outT = nc.dram_tensor("outT", [D, S], F32, kind="ExternalOutput").ap()
    xT3 = xT.rearrange("(c p) s -> p c s", p=128)
    oT3 = outT.rearrange("(c p) s -> p c s", p=128)

    with contextlib.ExitStack() as st:
        P = Prog(nc, st)

        def sb(name, shape, dt):
            return st.enter_context(nc.sbuf_tensor(name, shape, dt))

        need_kv = hi > 2
        if need_kv:
            Kc = sb("Kc", [128, NH, S], BF16)
            Vc = sb("Vc", [128, S // 128, NH * 128], BF16)
            KR = sb("KR", [128, S], BF16)
        h = sb("h", [128, 8, NT], F32)
        xn = sb("xn", [128, 8, NT], BF16)
        R = sb("R", [128, 12, NT], BF16)
        ring = sb("ring", [128, RING, 1024], BF16)
        B16 = sb("B16", [128, 3, NT], BF16)
        rr = sb("rr", [128, NT], F32)
        cs = sb("cs", [128, 2, NT], F32)
        TU = sb("TU", [128, 2, NT + 2], F32)
        TC = sb("TC", [128, 3, NT], F32)
        cst = sb("cst_sb", [128, 6 * 128], BF16)
        vec = sb("vec_sb", [128, NV], F32)
        halo_f = sb("halo_f", [128, 4 * 44, 2], F32)
        halo_a = sb("halo_a", [128, 2 * 8, 2], F32)
        ps = st.enter_context(nc.psum_tensor("ps", [128, 8, 512], F32))

        IDENT, NEGM, O1024, O512, O256, ONE = range(6)

        ring_ch = [P.chan("ring%d" % i) for i in range(RING)]
        ch_xs = [P.chan("ch_x%d" % i) for i in range(8)]
        ch_outs = [P.chan("ch_o%d" % i) for i in range(8)]
        ch_cs = P.chan("ch_cs")
        ch_vec = P.chan("ch_vec")
        ch_cst = P.chan("ch_cst")

        total_pieces = NP * ntiles
        state = {"pi": 0, "bank": 0, "b16": 0, "tu": 0, "tc": 0}
        held = set()

        def fetch(gi):
            if gi >= total_pieces:
                return
            k = gi % NP
            slot = gi % RING
            key, l, r0, kcs, cols = specs[k]
            n = kcs * len(cols)
            P.op("pool", lambda e: e.dma_start(out=ring[:, slot, 0:n], in_=wp[k, :, 0:n]),
                 writes=[("ring", slot)], chan=ring_ch[slot])

        def next_piece(expect_key):
            gi = state["pi"]
            state["pi"] += 1
            k = gi % NP
            assert specs[k][0] == expect_key, (specs[k][0], expect_key)
            return gi, gi % RING, specs[k]

        def bank(hold=False):
            for _ in range(16):
                b = state["bank"]
                state["bank"] = (b + 1) % 8
                if b not in held:
                    if hold:
                        held.add(b)
                    return b
            raise RuntimeError("no psum bank")

        def release(b):
            held.discard(b)

        def rot(name, n):
            v = state[name]
            state[name] = (v + 1) % n
            return v

        def vcol(key, c=0):
            o = voff[key] + c
            return vec[:, o:o + 1]

        def proj(expect_key, rhs_list, rhs_res, width=NT, hold=False, col0=0):
            gi, slot, spec = next_piece(expect_key)
            kcs = spec[3]
            assert kcs == len(rhs_list)
            b = bank(hold)
            for kc in range(kcs):
                rhs = rhs_list[kc]
                P.op("pe", (lambda e, kc=kc, rhs=rhs: e.matmul(
                    ps[:, b, col0:col0 + width], lhsT=ring[:, slot, kc * 128:(kc + 1) * 128], rhs=rhs,
                    start=(kc == 0), stop=(kc == kcs - 1))),
                    reads=[("ring", slot), rhs_res[kc]], writes=[("ps", b)])
            fetch(gi + RING)
            return b

        def warm(n):
            b = bank()
            for _ in range(n):
                P.op("pe", lambda e: e.matmul(ps[:, b, :], lhsT=cst[:, ONE * 128:(ONE + 1) * 128],
                                              rhs=cst[:, 0:512], start=True, stop=True),
                     reads=["cst"], writes=[("ps", b)])

        def norm(src_fn, src_res, nchunks, ones_idx, gain_key, dst_fn, dst_res, src_is_psum=False):
            bs = bank(True)
            for c in range(nchunks):
                q = rot("b16", 3)
                P.op("act", (lambda e, c=c, q=q: e.activation(out=B16[:, q, :], in_=src_fn(c), func=AF.Square)),
                     reads=[src_res(c)], writes=[("b16", q)])
                P.op("pe", (lambda e, c=c, q=q: e.matmul(ps[:, bs, :], lhsT=cst[:, ones_idx * 128:(ones_idx + 1) * 128], rhs=B16[:, q, :],
                                                         start=(c == 0), stop=(c == nchunks - 1))),
                     reads=[("b16", q), "cst"], writes=[("ps", bs)])
            P.op("act", lambda e: e.activation(out=rr[:], in_=ps[:, bs, :], func=AF.Sqrt, bias=EPS, scale=1.0),
                 reads=[("ps", bs)], writes=["rr"])
            release(bs)
            P.op("dve", lambda e: e.reciprocal(out=rr[:], in_=rr[:]), reads=["rr"], writes=["rr"])
            for c in range(nchunks):
                P.op("dve", (lambda e, c=c: e.scalar_tensor_tensor(
                    out=dst_fn(c), in0=src_fn(c), scalar=vcol(gain_key, c), in1=rr[:],
                    op0=ALU.mult, op1=ALU.mult)),
                    reads=[src_res(c), "rr", "vec"], writes=[dst_res(c)])
            warm(NWARM)

        h_fn = lambda c: h[:, c, :]
        h_res = lambda c: ("h", c)
        xn_fn = lambda c: xn[:, c, :]
        xn_res = lambda c: ("xn", c)
        xn_list = [xn[:, c, :] for c in range(8)]
        xn_rl = [("xn", c) for c in range(8)]

        def conv_from_psum(b, halo_ap, w0, w1, w2):
            u = rot("tu", 2)
            t = rot("tc", 3)
            P.op("act", lambda e: e.activation(out=TU[:, u, 0:2], in_=halo_ap, func=AF.Copy),
                 reads=["halo"], writes=[("tu", u)])
            P.op("act", lambda e: e.activation(out=TU[:, u, 2:NT + 2], in_=ps[:, b, :], func=AF.Copy),
                 reads=[("ps", b)], writes=[("tu", u)])
            P.op("act", lambda e: e.activation(out=halo_ap, in_=TU[:, u, NT:NT + 2], func=AF.Copy),
                 reads=[("tu", u)], writes=["halo"])
            P.op("act", lambda e: e.activation(out=TC[:, t, :], in_=ps[:, b, :], func=AF.Copy, scale=w2),
                 reads=[("ps", b), "vec"], writes=[("tc", t)])
            P.op("dve", lambda e: e.scalar_tensor_tensor(out=TC[:, t, :], in0=TU[:, u, 1:NT + 1], scalar=w1,
                                                         in1=TC[:, t, :], op0=ALU.mult, op1=ALU.add),
                 reads=[("tu", u), ("tc", t), "vec"], writes=[("tc", t)])
            P.op("dve", lambda e: e.scalar_tensor_tensor(out=TC[:, t, :], in0=TU[:, u, 0:NT], scalar=w0,
                                                         in1=TC[:, t, :], op0=ALU.mult, op1=ALU.add),
                 reads=[("tu", u), ("tc", t), "vec"], writes=[("tc", t)])
            return t

        def resid_add(b, m):
            P.op("dve", lambda e: e.tensor_tensor(out=h[:, m, :], in0=h[:, m, :], in1=ps[:, b, :], op=ALU.add),
                 reads=[("h", m), ("ps", b)], writes=[("h", m)])

        def ffn(l):
            norm(h_fn, h_res, 8, O1024, ("ffn_norm", l), xn_fn, xn_res)
            for kind, p in ffn_schedule():
                j0, j1 = FFN_PARTS[p]
                base = (p % 2) * 6
                if kind == "up":
                    for j in range(j0, j1):
                        tcs = []
                        for f in (j, 22 + j):
                            b = proj("ffn_w_up", xn_list, xn_rl)
                            hidx = l * 44 + f
                            tcs.append(conv_from_psum(b, halo_f[:, hidx, :],
                                                      vcol(("ffn_conv", l, 0), f), vcol(("ffn_conv", l, 1), f),
                                                      vcol(("ffn_conv", l, 2), f)))
                        tg, tu_ = tcs
                        P.op("act", lambda e, tg=tg: e.activation(out=TC[:, tg, :], in_=TC[:, tg, :], func=AF.Silu),
                             reads=[("tc", tg)], writes=[("tc", tg)])
                        jj = base + j - j0
                        P.op("dve", lambda e, tg=tg, tu_=tu_, jj=jj: e.tensor_tensor(
                            out=R[:, jj, :], in0=TC[:, tg, :], in1=TC[:, tu_, :], op=ALU.mult),
                            reads=[("tc", tg), ("tc", tu_)], writes=[("R", jj)])
                else:
                    n = j1 - j0
                    if p == len(FFN_PARTS) - 1:
                        warm(10)
                    for m in range(8):
                        b = proj("ffn_w_down", [R[:, base + k, :] for k in range(n)],
                                 [("R", base + k) for k in range(n)])
                        resid_add(b, m)

        def a_mixer(l):
            norm(h_fn, h_res, 8, O1024, ("a_mix_norm", l), xn_fn, xn_res)
            for i in range(8):
                bc = proj("a_w_in", xn_list, xn_rl)
                bh = proj("a_w_in", xn_list, xn_rl)
                bb = proj("a_w_in", xn_list, xn_rl)
                t0 = rot("tc", 3)
                u = rot("tu", 2)
                hal = halo_a[:, l * 8 + i, :]
                P.op("act", lambda e, t0=t0, bc=bc: e.activation(out=TC[:, t0, :], in_=ps[:, bc, :], func=AF.Copy),
                     reads=[("ps", bc)], writes=[("tc", t0)])
                P.op("act", lambda e, u=u, hal=hal: e.activation(out=TU[:, u, 0:2], in_=hal, func=AF.Copy),
                     reads=["halo"], writes=[("tu", u)])
                P.op("dve", lambda e, u=u, t0=t0, bh=bh: e.tensor_tensor(
                    out=TU[:, u, 2:NT + 2], in0=TC[:, t0, :], in1=ps[:, bh, :], op=ALU.mult),
                    reads=[("tc", t0), ("ps", bh)], writes=[("tu", u)])
                P.op("act", lambda e, u=u, hal=hal: e.activation(out=hal, in_=TU[:, u, NT:NT + 2], func=AF.Copy),
                     reads=[("tu", u)], writes=["halo"])
                t1 = rot("tc", 3)
                w0, w1, w2 = (vcol(("a_conv", l, tap), i) for tap in range(3))
                P.op("act", lambda e, u=u, t1=t1, w2=w2: e.activation(
                    out=TC[:, t1, :], in_=TU[:, u, 2:NT + 2], func=AF.Copy, scale=w2),
                    reads=[("tu", u), "vec"], writes=[("tc", t1)])
                P.op("dve", lambda e, u=u, t1=t1, w1=w1: e.scalar_tensor_tensor(
                    out=TC[:, t1, :], in0=TU[:, u, 1:NT + 1], scalar=w1, in1=TC[:, t1, :],
                    op0=ALU.mult, op1=ALU.add),
                    reads=[("tu", u), ("tc", t1), "vec"], writes=[("tc", t1)])
                P.op("dve", lambda e, u=u, t1=t1, w0=w0: e.scalar_tensor_tensor(
                    out=TC[:, t1, :], in0=TU[:, u, 0:NT], scalar=w0, in1=TC[:, t1, :],
                    op0=ALU.mult, op1=ALU.add),
                    reads=[("tu", u), ("tc", t1), "vec"], writes=[("tc", t1)])
                P.op("dve", lambda e, t1=t1, bb=bb, i=i: e.tensor_tensor(
                    out=R[:, i, :], in0=TC[:, t1, :], in1=ps[:, bb, :], op=ALU.mult),
                    reads=[("tc", t1), ("ps", bb)], writes=[("R", i)])
            warm(8)
            for m in range(8):
                b = proj("a_w_out", [R[:, k, :] for k in range(8)], [("R", k) for k in range(8)])
                resid_add(b, m)

        def rope_combine(b_r, b_t, out_ap, out_res):
            ta = rot("tc", 3)
            tb = rot("tc", 3)
            P.op("dve", lambda e: e.tensor_tensor(out=TC[:, ta, :], in0=cs[:, 0, :], in1=ps[:, b_r, :], op=ALU.mult),
                 reads=["cs", ("ps", b_r)], writes=[("tc", ta)])
            P.op("dve", lambda e: e.tensor_tensor(out=TC[:, tb, :], in0=cs[:, 1, :], in1=ps[:, b_t, :], op=ALU.mult),
                 reads=["cs", ("ps", b_t)], writes=[("tc", tb)])
            P.op("dve", lambda e: e.tensor_tensor(out=out_ap, in0=TC[:, ta, :], in1=TC[:, tb, :], op=ALU.add),
                 reads=[("tc", ta), ("tc", tb)], writes=[out_res])

        def kv_stage(t):
            c0 = t * NT
            norm(h_fn, h_res, 8, O1024, ("kv_in_norm", None), xn_fn, xn_res)
            b0 = proj("w_dkv", xn_list, xn_rl, hold=True)
            b1 = proj("w_dkv", xn_list, xn_rl, hold=True)
            br = proj("w_dkv", xn_list, xn_rl, hold=True)
            bt = proj("w_dkv", xn_list, xn_rl, hold=True)
            rope_combine(br, bt, KR[:, c0:c0 + NT], ("KR", t))
            release(br)
            release(bt)
            bl = [b0, b1]
            norm(lambda c: ps[:, bl[c], :], lambda c: ("ps", bl[c]), 2, O256, ("kv_norm", None),
                 lambda c: R[:, c, :], lambda c: ("R", c))
            release(b0)
            release(b1)
            ck = [R[:, c, :] for c in range(2)]
            ckr = [("R", c) for c in range(2)]
            for hh in range(NH):
                b = proj("w_ukv", ck, ckr)
                P.op("act", lambda e, b=b, hh=hh: e.activation(out=Kc[:, hh, c0:c0 + NT], in_=ps[:, b, :], func=AF.Copy),
                     reads=[("ps", b)], writes=[("Kc", hh, t)])
            for g in range(2):
                gi, slot, spec = next_piece("w_ukv")
                for s4 in range(4):
                    b = bank()
                    for kc in range(2):
                        P.op("pe", lambda e, b=b, kc=kc, s4=s4, slot=slot: e.matmul(
                            ps[:, b, :], lhsT=R[:, kc, s4 * 128:(s4 + 1) * 128],
                            rhs=ring[:, slot, kc * 512:(kc + 1) * 512], start=(kc == 0), stop=(kc == 1)),
                            reads=[("ring", slot), ("R", kc)], writes=[("ps", b)])
                    eng = "act" if (s4 % 2 == 0) else "dve"
                    if eng == "act":
                        P.op("act", lambda e, b=b, s4=s4, g=g: e.activation(
                            out=Vc[:, 4 * t + s4, g * 512:(g + 1) * 512], in_=ps[:, b, :], func=AF.Copy),
                            reads=[("ps", b)], writes=[("Vc", 4 * t + s4, g)])
                    else:
                        P.op("dve", lambda e, b=b, s4=s4, g=g: e.tensor_copy(
                            out=Vc[:, 4 * t + s4, g * 512:(g + 1) * 512], in_=ps[:, b, :]),
                            reads=[("ps", b)], writes=[("Vc", 4 * t + s4, g)])
                fetch(gi + RING)

        def attention(t):
            nkc = 4 * (t + 1)

            def head(hh):
                p, half = hh // 2, hh % 2
                r0 = half * 64
                bo = bank(True)
                bd = bank(True)
                sb_ = {}
                pb_ = {}

                def s_mm(kc):
                    i = kc - 4 * t
                    q0 = max(0, i) * 128
                    b = bank(True)
                    sb_[kc] = b
                    diag = i >= 0
                    P.op("pe", lambda e: e.matmul(ps[:, b, q0:NT], lhsT=Kc[:, hh, kc * 128:(kc + 1) * 128],
                                                  rhs=R[:, hh, q0:NT], start=True, stop=False),
                         reads=[("Kc", hh, kc // 4), ("R", hh)], writes=[("ps", b)])
                    P.op("pe", lambda e: e.matmul(ps[:, b, q0:NT], lhsT=KR[r0:r0 + 64, kc * 128:(kc + 1) * 128],
                                                  rhs=R[r0:r0 + 64, 8 + p, q0:NT], start=False, stop=not diag),
                         reads=[("KR", kc // 4), ("R", 8 + p)], writes=[("ps", b)])
                    if diag:
                        P.op("pe", lambda e: e.matmul(ps[:, b, q0:q0 + 128], lhsT=cst[:, IDENT * 128:(IDENT + 1) * 128],
                                                      rhs=cst[:, NEGM * 128:(NEGM + 1) * 128], start=False, stop=True),
                             reads=["cst"], writes=[("ps", b)])

                def e_act(kc):
                    i = kc - 4 * t
                    q0 = max(0, i) * 128
                    b = sb_[kc]
                    q = rot("b16", 3)
                    pb_[kc] = q
                    P.op("act", lambda e: e.activation(out=B16[:, q, q0:NT], in_=ps[:, b, q0:NT], func=AF.Exp,
                                                       scale=SCALE),
                         reads=[("ps", b)], writes=[("b16", q)])
                    release(b)

                def pv_mm(kc):
                    i = kc - 4 * t
                    q0 = max(0, i) * 128
                    q = pb_[kc]
                    P.op("pe", lambda e: e.matmul(ps[:, bo, q0:NT], lhsT=Vc[:, kc, hh * 128:(hh + 1) * 128],
                                                  rhs=B16[:, q, q0:NT], start=(kc == 0), stop=(kc == nkc - 1)),
                         reads=[("Vc", kc, hh // 4), ("b16", q)], writes=[("ps", bo)])
                    P.op("pe", lambda e: e.matmul(ps[:, bd, q0:NT], lhsT=cst[:, ONE * 128:(ONE + 1) * 128],
                                                  rhs=B16[:, q, q0:NT], start=(kc == 0), stop=(kc == nkc - 1)),
                         reads=["cst", ("b16", q)], writes=[("ps", bd)])

                s_mm(0)
                if nkc > 1:
                    s_mm(1)
                for kc in range(nkc):
                    e_act(kc)
                    pv_mm(kc)
                    if kc + 2 < nkc:
                        s_mm(kc + 2)
                P.op("dve", lambda e: e.reciprocal(out=rr[:], in_=ps[:, bd, :]), reads=[("ps", bd)], writes=["rr"])
                P.op("dve", lambda e: e.tensor_tensor(out=xn[:, hh, :], in0=rr[:], in1=ps[:, bo, :], op=ALU.mult),
                     reads=["rr", ("ps", bo)], writes=[("xn", hh)])
                release(bo)
                release(bd)

            for hh in range(NH):
                head(hh)

        def b_mixer(j, t):
            norm(h_fn, h_res, 8, O1024, ("b_mix_norm", j), xn_fn, xn_res)
            bq = [proj("b_w_dq", xn_list, xn_rl, hold=True) for _ in range(4)]
            norm(lambda c: ps[:, bq[c], :], lambda c: ("ps", bq[c]), 4, O512, ("b_q_norm", j),
                 lambda c: xn[:, c, :], lambda c: ("xn", c))
            for b in bq:
                release(b)
            ql = [xn[:, c, :] for c in range(4)]
            qlr = [("xn", c) for c in range(4)]
            for hh in range(NH):
                b = proj("b_w_uq", ql, qlr)
                P.op("act", lambda e, b=b, hh=hh: e.activation(out=R[:, hh, :], in_=ps[:, b, :], func=AF.Copy),
                     reads=[("ps", b)], writes=[("R", hh)])
            brs = [proj("b_w_uq", ql, qlr, hold=True) for _ in range(4)]
            for p in range(4):
                bt = proj("b_w_uq", ql, qlr)
                rope_combine(brs[p], bt, R[:, 8 + p, :], ("R", 8 + p))
                release(brs[p])
            attention(t)
            warm(14)
            for m in range(8):
                b = proj("b_w_o", xn_list, xn_rl)
                resid_add(b, m)

        P.op("sp", lambda e: e.dma_start(out=vec[:], in_=vecd), writes=["vec"], chan=ch_vec)
        P.op("pool", lambda e: e.dma_start(out=cst[:, 0:256], in_=cstd),
             writes=["cst"], chan=ch_cst)
        for idx, val in ((O1024, 1.0 / 1024), (O512, 1.0 / 512), (O256, 1.0 / 256), (ONE, 1.0)):
            P.op("dve", lambda e, idx=idx, val=val: e.memset(cst[:, idx * 128:(idx + 1) * 128], val), writes=["cst"])
        P.op("dve", lambda e: e.memset(halo_f[:], 0.0), writes=["halo"])
        P.op("dve", lambda e: e.memset(halo_a[:], 0.0), writes=["halo"])
        for gi in range(RING):
            fetch(gi)

        for t in range(ntiles):
            c0 = t * NT
            for c in range(8):
                P.op("sp", lambda e, c0=c0, c=c: e.dma_start(out=h[:, c, :], in_=xT3[:, c, c0:c0 + NT]),
                     writes=[("h", c)], chan=ch_xs[c])
            if need_kv:
                P.op("sp", lambda e, c0=c0: e.dma_start(out=cs[:], in_=ropd[:, :, c0:c0 + NT].rearrange("a p s -> p a s")),
                     writes=["cs"], chan=ch_cs)
            for layer in range(lo, hi):
                if layer < 2:
                    a_mixer(layer)
                else:
                    if layer == 2:
                        kv_stage(t)
                    b_mixer(layer - 2, t)
                ffn(layer)
            if hi == 4:
                norm(h_fn, h_res, 8, O1024, ("final_norm", None), h_fn, h_res)
            for c in range(8):
                P.op("sp", lambda e, c0=c0, c=c: e.dma_start(out=oT3[:, c, c0:c0 + NT], in_=h[:, c, :]),
                     reads=[("h", c)], chan=ch_outs[c])
        assert state["pi"] == total_pieces, (state["pi"], total_pieces)
        lasts = P.streams["sp"][-8:]
        fin = P.op("sp", lambda e: e.nop())
        fin.deps = list(lasts)
        P.emit()
    return nc


def _host_inputs(inputs, lo, hi):
    specs = piece_list(lo, hi)
    w = {k: np.asarray(v, np.float32) for k, v in inputs.items() if k != "x"}
    wp = pack_pieces(specs, w)
    vec = pack_vec(w)
    rope = rope_tables()
    cst = const_tables()
    return wp, vec, rope, cst


def kernel(**inputs):
    x = np.asarray(inputs["x"], np.float32)
    B = x.shape[0]
    wp, vec, rope, cst = _host_inputs(inputs, 0, 4)
    nc = build_nc(0, 4, S // NT)
    in_maps = []
    for b in range(B):
        in_maps.append({"xT": np.ascontiguousarray(x[b].T), "wp": wp, "vec": vec, "rope": rope, "cst": cst})
    res = run_bass_kernel_spmd(nc, in_maps, core_ids=list(range(B)))
    out = np.stack([np.ascontiguousarray(res.results[b]["outT"].T) for b in range(B)], axis=0)
    return out.astype(np.float32)
```
